# Optimizing a Trainium2 kernel written in Bass

```python
import jax, jax.numpy as jnp
from jax import lax
import numpy as np

D_MODEL = 2048
BATCH = 16
SEQ = 2048
DEPTH = 4
DEC_BATCH = 2
DEC_SEQ = 8192
PAST_LEN = 128

D_MIX = D_MODEL
W_FOURIER = D_MIX // 4
W_CONFORMER = D_MIX // 4
W_ATTN = D_MIX // 4
W_SHORTCONV = D_MIX - W_FOURIER - W_CONFORMER - W_ATTN
N_FOURIER_GROUPS = 4
FOURIER_GROUP = W_FOURIER // N_FOURIER_GROUPS
CONF_KERNEL = 31
HEAD_DIM = 64
N_ATTN_HEADS = W_ATTN // HEAD_DIM
DILATED_CONFIGS = ((128, 1), (512, 4), (2048, 16))
SHORT_KERNEL = 3
D_FF = 5504
RMS_EPS = 1e-6
LN_EPS = 1e-5
NEG_BIG = -1e30
IN_SIZES = (W_FOURIER, W_CONFORMER, W_CONFORMER, W_ATTN, W_ATTN, W_ATTN,
            W_SHORTCONV, W_SHORTCONV, W_SHORTCONV)
IN_COLS = sum(IN_SIZES)
IN_SPLITS = tuple(int(i) for i in np.cumsum(IN_SIZES)[:-1])

kernel_name = "hybrid_parallel_fourier_conformer_dilated_shortconv_encoder"


def rms_norm(x, g):
    xf = x.astype(jnp.float32)
    y = xf * lax.rsqrt(jnp.mean(xf * xf, axis=-1, keepdims=True) + RMS_EPS)
    return (y * g.astype(jnp.float32)).astype(x.dtype)


def layer_norm(x, g, b):
    xf = x.astype(jnp.float32)
    mu = jnp.mean(xf, axis=-1, keepdims=True)
    var = jnp.mean(jnp.square(xf - mu), axis=-1, keepdims=True)
    y = (xf - mu) * lax.rsqrt(var + LN_EPS) * g.astype(jnp.float32) + b.astype(jnp.float32)
    return y.astype(x.dtype)


def swiglu(x, wg, wu, wd):
    return (jax.nn.silu(x @ wg) * (x @ wu)) @ wd


def depthwise_conv(x, w):
    k = w.shape[0]
    c = x.shape[-1]
    return lax.conv_general_dilated(
        x, w.astype(x.dtype)[:, None, :], window_strides=(1,),
        padding=[(k // 2, k // 2)], dimension_numbers=("NWC", "WIO", "NWC"),
        feature_group_count=c)


def fourier_mix(u, w_f):
    b, s, _ = u.shape
    uf = u.astype(jnp.float32).reshape(b, s, N_FOURIER_GROUPS, FOURIER_GROUP)
    f = jnp.fft.fft2(uf, axes=(1, 3), norm="ortho").real
    y = jnp.einsum("bsgc,gce->bsge", f, w_f.astype(jnp.float32))
    return y.reshape(b, s, W_FOURIER).astype(u.dtype)


def conformer_conv(val, gate, w_dw, b_dw, ln_g, ln_b):
    h = val * jax.nn.sigmoid(gate)
    h = depthwise_conv(h, w_dw) + b_dw.astype(h.dtype)
    h = layer_norm(h, ln_g, ln_b)
    return jax.nn.silu(h)


def dilated_branch(q, k, v, window, dil, slopes):
    b, s, h, e = q.shape
    half = window // (2 * dil)
    l = s // dil
    nb = -(-l // half)
    lp = nb * half
    q_r = q.reshape(b, l, dil, h, e)
    k_r = k.reshape(b, l, dil, h, e)
    v_r = v.reshape(b, l, dil, h, e)
    zpad = ((0, 0), (0, 0), (0, 0))
    q_r = jnp.pad(q_r, ((0, 0), (0, lp - l)) + zpad)
    k_r = jnp.pad(k_r, ((0, 0), (half, lp - l + half)) + zpad)
    v_r = jnp.pad(v_r, ((0, 0), (half, lp - l + half)) + zpad)
    qb = q_r.reshape(b, nb, half, dil, h, e)
    kb = k_r.reshape(b, nb + 2, half, dil, h, e)
    vb = v_r.reshape(b, nb + 2, half, dil, h, e)
    kw = jnp.concatenate([kb[:, :-2], kb[:, 1:-1], kb[:, 2:]], axis=2)
    vw = jnp.concatenate([vb[:, :-2], vb[:, 1:-1], vb[:, 2:]], axis=2)
    sc = jnp.einsum("bnqrhe,bnkrhe->bnrhqk", qb, kw) * (e ** -0.5)
    offs = jnp.arange(3 * half)[None, :] - half - jnp.arange(half)[:, None]
    key_pos = jnp.arange(nb)[:, None] * half + jnp.arange(3 * half)[None, :] - half
    valid = (jnp.abs(offs) <= half)[None] & ((key_pos >= 0) & (key_pos < l))[:, None, :]
    alibi = -slopes[:, None, None] * (dil * jnp.abs(offs)).astype(jnp.float32)[None]
    sc = jnp.where(valid[None, :, None, None], sc + alibi[None, None, None], NEG_BIG)
    m = jnp.max(sc, axis=-1)
    p = jnp.exp(sc - m[..., None])
    den = jnp.sum(p, axis=-1)
    num = jnp.einsum("bnrhqk,bnkrhe->bnqrhe", p, vw)
    num = num.reshape(b, lp, dil, h, e)[:, :l].reshape(b, s, h, e)
    m = jnp.transpose(m, (0, 1, 4, 2, 3)).reshape(b, lp, dil, h)[:, :l].reshape(b, s, h)
    den = jnp.transpose(den, (0, 1, 4, 2, 3)).reshape(b, lp, dil, h)[:, :l].reshape(b, s, h)
    return num, m, den


def dilated_attention(q, k, v):
    b, s, _ = q.shape
    dt = q.dtype
    qh = q.astype(jnp.float32).reshape(b, s, N_ATTN_HEADS, HEAD_DIM)
    kh = k.astype(jnp.float32).reshape(b, s, N_ATTN_HEADS, HEAD_DIM)
    vh = v.astype(jnp.float32).reshape(b, s, N_ATTN_HEADS, HEAD_DIM)
    slopes = jnp.exp2(-8.0 * (jnp.arange(N_ATTN_HEADS, dtype=jnp.float32) + 1.0) / N_ATTN_HEADS)
    parts = [dilated_branch(qh, kh, vh, w, d, slopes) for (w, d) in DILATED_CONFIGS]
    m_all = jnp.max(jnp.stack([pm for (_, pm, _) in parts], axis=0), axis=0)
    num = sum(jnp.exp(pm - m_all)[..., None] * pn for (pn, pm, _) in parts)
    den = sum(jnp.exp(pm - m_all) * pd for (_, pm, pd) in parts)
    o = num / den[..., None]
    return o.reshape(b, s, W_ATTN).astype(dt)


def token_mixing(h, w_in, w_fourier, conv_b_w, conv_b_bias, ln_conv_gain, ln_conv_bias, conv_d_w, w_out):
    u = h @ w_in
    ua, cv, cg, q, k, v, sb, sc, sx = jnp.split(u, IN_SPLITS, axis=-1)
    y_a = fourier_mix(ua, w_fourier)
    y_b = conformer_conv(cv, cg, conv_b_w, conv_b_bias, ln_conv_gain, ln_conv_bias)
    y_c = dilated_attention(q, k, v)
    y_d = sb * depthwise_conv(sc * sx, conv_d_w)
    y = jnp.concatenate([y_a, y_b, y_c, y_d], axis=-1)
    return y @ w_out


def trunk(x, ln_ffn1, w_ffn1_gate, w_ffn1_up, w_ffn1_down, ln_mix, w_in, w_fourier, conv_b_w,
          conv_b_bias, ln_conv_gain, ln_conv_bias, conv_d_w, w_out, ln_ffn2, w_ffn2_gate,
          w_ffn2_up, w_ffn2_down, ln_final):
    for i in range(DEPTH):
        x = x + 0.5 * swiglu(rms_norm(x, ln_ffn1[i]), w_ffn1_gate[i], w_ffn1_up[i], w_ffn1_down[i])
        x = x + token_mixing(rms_norm(x, ln_mix[i]), w_in[i], w_fourier[i], conv_b_w[i], conv_b_bias[i],
                             ln_conv_gain[i], ln_conv_bias[i], conv_d_w[i], w_out[i])
        x = x + 0.5 * swiglu(rms_norm(x, ln_ffn2[i]), w_ffn2_gate[i], w_ffn2_up[i], w_ffn2_down[i])
    return rms_norm(x, ln_final)


def setup_inputs(seed: int = 0) -> dict:
    key = jax.random.key(seed)
    ks = jax.random.split(key, 22)
    nrm = jax.random.normal
    f32 = jnp.float32
    def gain(k, shape):
        return 1.0 + 0.02 * nrm(k, shape, f32)
    return {
        "x_prompt": nrm(ks[0], (BATCH, SEQ, D_MODEL), f32),
        "x_sample": nrm(ks[1], (DEC_BATCH, DEC_SEQ, D_MODEL), f32),
        "ln_ffn1": gain(ks[2], (DEPTH, D_MODEL)),
        "w_ffn1_gate": nrm(ks[3], (DEPTH, D_MODEL, D_FF), f32) * D_MODEL ** -0.5,
        "w_ffn1_up": nrm(ks[4], (DEPTH, D_MODEL, D_FF), f32) * D_MODEL ** -0.5,
        "w_ffn1_down": nrm(ks[5], (DEPTH, D_FF, D_MODEL), f32) * D_FF ** -0.5,
        "ln_mix": gain(ks[6], (DEPTH, D_MODEL)),
        "w_in": nrm(ks[7], (DEPTH, D_MODEL, IN_COLS), f32) * D_MODEL ** -0.5,
        "w_fourier": nrm(ks[8], (DEPTH, N_FOURIER_GROUPS, FOURIER_GROUP, FOURIER_GROUP), f32) * FOURIER_GROUP ** -0.5,
        "conv_b_w": nrm(ks[9], (DEPTH, CONF_KERNEL, W_CONFORMER), f32) * CONF_KERNEL ** -0.5,
        "conv_b_bias": 0.02 * nrm(ks[10], (DEPTH, W_CONFORMER), f32),
        "ln_conv_gain": gain(ks[11], (DEPTH, W_CONFORMER)),
        "ln_conv_bias": 0.02 * nrm(ks[12], (DEPTH, W_CONFORMER), f32),
        "conv_d_w": nrm(ks[13], (DEPTH, SHORT_KERNEL, W_SHORTCONV), f32) * SHORT_KERNEL ** -0.5,
        "w_out": nrm(ks[14], (DEPTH, D_MIX, D_MODEL), f32) * D_MIX ** -0.5,
        "ln_ffn2": gain(ks[15], (DEPTH, D_MODEL)),
        "w_ffn2_gate": nrm(ks[16], (DEPTH, D_MODEL, D_FF), f32) * D_MODEL ** -0.5,
        "w_ffn2_up": nrm(ks[17], (DEPTH, D_MODEL, D_FF), f32) * D_MODEL ** -0.5,
        "w_ffn2_down": nrm(ks[18], (DEPTH, D_FF, D_MODEL), f32) * D_FF ** -0.5,
        "ln_final": gain(ks[19], (D_MODEL,)),
    }


def reference(x_prompt, x_sample, ln_ffn1, w_ffn1_gate, w_ffn1_up, w_ffn1_down, ln_mix, w_in, w_fourier,
              conv_b_w, conv_b_bias, ln_conv_gain, ln_conv_bias, conv_d_w, w_out, ln_ffn2, w_ffn2_gate,
              w_ffn2_up, w_ffn2_down, ln_final):
    y_prompt = trunk(x_prompt, ln_ffn1, w_ffn1_gate, w_ffn1_up, w_ffn1_down, ln_mix, w_in, w_fourier,
                     conv_b_w, conv_b_bias, ln_conv_gain, ln_conv_bias, conv_d_w, w_out, ln_ffn2,
                     w_ffn2_gate, w_ffn2_up, w_ffn2_down, ln_final)
    y_sample = trunk(x_sample, ln_ffn1, w_ffn1_gate, w_ffn1_up, w_ffn1_down, ln_mix, w_in, w_fourier,
                     conv_b_w, conv_b_bias, ln_conv_gain, ln_conv_bias, conv_d_w, w_out, ln_ffn2,
                     w_ffn2_gate, w_ffn2_up, w_ffn2_down, ln_final)
    return (y_prompt, y_sample)
```

```python
import math
from contextlib import ExitStack
import numpy as np
import ml_dtypes
import concourse.bass as bass
import concourse.mybir as mybir
from concourse.bass_utils import run_bass_kernel_spmd

F32 = mybir.dt.float32
BF16 = mybir.dt.bfloat16
AF = mybir.ActivationFunctionType
ALU = mybir.AluOpType
AX = mybir.AxisListType

D = 2048
KC = D // 128
L = 2048
WM = 512
IN_COLS = 9 * WM
CONF_K = 31
RMS_EPS = 1e-6
LN_EPS = 1e-5
DILS = (1, 4, 16)
NSLOPE = 12


class Buf:
    __slots__ = ("name", "w", "r", "acc")

    def __init__(self, name, acc=False):
        self.name = name
        self.w = {}
        self.r = {}
        self.acc = acc


class Queue:
    def __init__(self, eng, sems, is_dma, is_pe=False):
        self.eng = eng
        self.sems = sems
        self.cnt = [0] * len(sems)
        self.is_dma = is_dma
        self.is_pe = is_pe
        self.seen = {}
        self.rr = 0
        self.ops = []


class K:
    def __init__(self, nc, stack, n_sync=24, n_pool=12):
        self.nc = nc
        self.q = {}

        def mk(n, name):
            return [stack.enter_context(nc.semaphore(f"{name}{i}")) for i in range(n)]

        self.q["pe"] = Queue(nc.tensor, mk(1, "spe"), False, True)
        self.q["act"] = Queue(nc.scalar, mk(1, "sact"), False)
        self.q["dve"] = Queue(nc.vector, mk(1, "sdve"), False)
        self.q["sync"] = Queue(nc.sync, mk(n_sync, "ssy"), True)
        self.q["pool"] = Queue(nc.gpsimd, mk(n_pool, "spl"), True)
        self.nins = 0

    def _wait(self, q, sem, val):
        if q.seen.get(sem, 0) >= val:
            return
        q.ops.append((None, sem, val))
        q.seen[sem] = val

    def replay(self, block):
        def run(q):
            def body(eng):
                for fn, sem, val in q.ops:
                    if fn is None:
                        eng.wait_ge(sem, val)
                    else:
                        ins = getattr(eng, fn[0])(**fn[1])
                        if val:
                            ins.then_inc(sem, val)
            return body
        block.tensor(run(self.q["pe"]))
        block.scalar(run(self.q["act"]))
        block.vector(run(self.q["dve"]))
        block.sync(run(self.q["sync"]))
        block.gpsimd(run(self.q["pool"]))

    def emit(self, qn, meth, kw, reads=(), writes=(), sig=True):
        fn = (meth, kw)
        q = self.q[qn]
        deps = {}
        for b in reads:
            for s, v in b.w.items():
                if deps.get(s, 0) < v:
                    deps[s] = v
        for b in writes:
            for s, v in b.w.items():
                if deps.get(s, 0) < v:
                    deps[s] = v
            for s, v in b.r.items():
                if deps.get(s, 0) < v:
                    deps[s] = v
        own = q.sems[0] if not q.is_dma else None
        for s, v in deps.items():
            if q.is_pe and s is own:
                continue
            self._wait(q, s, v)
        if q.is_dma:
            i = q.rr
            q.rr = (q.rr + 1) % len(q.sems)
            sem = q.sems[i]
            if q.cnt[i]:
                self._wait(q, sem, q.cnt[i])
            q.ops.append((fn, sem, 16))
            q.cnt[i] += 16
            ev = (sem, q.cnt[i])
        else:
            sem = q.sems[0]
            if sig:
                q.cnt[0] += 1
                q.ops.append((fn, sem, 1))
                ev = (sem, q.cnt[0])
            else:
                q.ops.append((fn, sem, 0))
                ev = (sem, q.cnt[0] + 1)
        self.nins += 1
        for b in reads:
            if b.r.get(ev[0], 0) < ev[1]:
                b.r[ev[0]] = ev[1]
        for b in writes:
            if b.acc:
                if b.w.get(ev[0], 0) < ev[1]:
                    b.w[ev[0]] = ev[1]
            else:
                b.w = {ev[0]: ev[1]}
                b.r = {}
        return ev

    def finish(self, bufs):
        q = self.q["sync"]
        for b in bufs:
            for s, v in b.w.items():
                self._wait(q, s, v)


class Slots:
    def __init__(self, items):
        self.items = items
        self.i = 0

    def next(self):
        it = self.items[self.i]
        self.i = (self.i + 1) % len(self.items)
        return it


def ff_slabs(dff, w):
    out = []
    c = 0
    while c < dff:
        ww = min(w, dff - c)
        out.append((c, ww))
        c += ww
    return out


def switch_alias(old_bufs, new_bufs):
    merged = {}
    for b in old_bufs:
        for dd in (b.w, b.r):
            for s, v in dd.items():
                if merged.get(s, 0) < v:
                    merged[s] = v
    for b in new_bufs:
        b.w = dict(merged)
        b.r = {}


def build(cfg):
    NSEG = cfg["NSEG"]
    DEPTH = cfg["DEPTH"]
    DFF = cfg["DFF"]
    TOK = NSEG * L
    TT = 1024
    NT = TOK // TT
    FC = DFF // 128
    stages = cfg.get("stages", "f1,mix,f2")
    mixers = cfg.get("mixers", "abcd")

    nc = bass.Bass("TRN2", target_bir_lowering=False)

    def din(name, shape, dt=F32):
        return nc.dram_tensor(name, list(shape), dt, kind="ExternalInput").ap()

    x_in = din("x", [TOK, D])
    y_out = nc.dram_tensor("y", [TOK, D], F32, kind="ExternalOutput").ap()
    ln_ffn1 = din("ln_ffn1", [DEPTH, D])
    ln_mix = din("ln_mix", [DEPTH, D])
    ln_ffn2 = din("ln_ffn2", [DEPTH, D])
    ln_final = din("ln_final", [1, D])
    w1g = din("w_ffn1_gate", [DEPTH, D, DFF])
    w1u = din("w_ffn1_up", [DEPTH, D, DFF])
    w1d = din("w_ffn1_down", [DEPTH, DFF, D])
    w2g = din("w_ffn2_gate", [DEPTH, D, DFF])
    w2u = din("w_ffn2_up", [DEPTH, D, DFF])
    w2d = din("w_ffn2_down", [DEPTH, DFF, D])
    w_in = din("w_in", [DEPTH, D, IN_COLS])
    w_out = din("w_out", [DEPTH, D, D])
    w_fourier = din("w_fourier", [DEPTH, 4, 128, 128])
    conv_b_w = din("conv_b_w", [DEPTH, CONF_K, WM])
    conv_b_bias = din("conv_b_bias", [DEPTH, WM])
    ln_conv_gain = din("ln_conv_gain", [DEPTH, WM])
    ln_conv_bias = din("ln_conv_bias", [DEPTH, WM])
    conv_d_w = din("conv_d_w", [DEPTH, 3, WM])
    c_ident = din("c_ident", [128, 128], BF16)
    c_dftc = din("c_dftc", [TOK, TOK], BF16)
    c_dfts = din("c_dfts", [TOK, TOK], BF16)
    c_cc = din("c_cc", [128, 128])
    c_sc = din("c_sc", [128, 128])
    c_mask = din("c_mask", [128, NSLOPE * 3, 128])
    c_flags = din("c_flags", [128, 2 * NSEG])

    def dscr(name, shape, dt):
        return nc.dram_tensor(name, list(shape), dt, kind="Internal").ap()

    PADC = 16
    s_pq = dscr("s_pq", [TOK, 1024], BF16)
    s_hb = dscr("s_hb", [WM, TOK + 2 * PADC], BF16)
    s_q = dscr("s_q", [WM, TOK], BF16)
    s_k = dscr("s_k", [WM, TOK], BF16)
    s_v = dscr("s_v", [TOK, WM], BF16)
    s_sb = dscr("s_sb", [WM, TOK], BF16)
    s_p = dscr("s_p", [WM, TOK + 2 * PADC], BF16)
    b_spq = [Buf(f"spq{i}") for i in range(NT)]
    b_shb = [Buf(f"shb{i}") for i in range(NT)]
    b_sq = [Buf(f"sq{i}") for i in range(NT)]
    b_sk = [Buf(f"sk{i}") for i in range(NT)]
    b_sv = [Buf(f"sv{i}") for i in range(NT)]
    b_ssb = [Buf(f"ssb{i}") for i in range(NT)]
    b_sp = [Buf(f"sp{i}") for i in range(NT)]
    b_pad = Buf("pads")

    with ExitStack() as st:
        k = K(nc, st)
        E = k.emit

        def sb(name, shape, dt):
            return st.enter_context(nc.sbuf_tensor(name, list(shape), dt))

        ps = st.enter_context(nc.psum_tensor("ps", [128, 8, 512], F32))
        psb = [Buf(f"ps{i}") for i in range(8)]
        pstate = {"i": 0}

        def bank():
            i = pstate["i"]
            pstate["i"] = (i + 1) % 8
            return i

        def banks2():
            i = pstate["i"]
            if i % 2:
                i = (i + 1) % 8
            pstate["i"] = (i + 2) % 8
            return i

        def dram_written(buf):
            q = k.q["sync"]
            i = (q.rr - 1) % len(q.sems)
            sem, v = q.sems[i], q.cnt[i]
            if buf.w.get(sem, 0) < v:
                buf.w[sem] = v

        ident = sb("ident", [128, 128], BF16)
        b_const = Buf("const", acc=True)
        E("sync", "dma_start", dict(out=ident[:], in_=c_ident), writes=[b_const])
        ones_bf = sb("ones_bf", [128, 128], BF16)
        ones_f = sb("ones_f", [128, 128], F32)
        E("dve", "memset", dict(ap=ones_bf[:], constant=1.0), writes=[b_const])
        E("dve", "memset", dict(ap=ones_f[:], constant=1.0), writes=[b_const])
        cc_t = sb("cc_t", [128, 128], F32)
        sc_t = sb("sc_t", [128, 128], F32)
        E("sync", "dma_start", dict(out=cc_t[:], in_=c_cc), writes=[b_const])
        E("sync", "dma_start", dict(out=sc_t[:], in_=c_sc), writes=[b_const])
        flags = sb("flags", [128, 2 * NSEG], F32)
        E("sync", "dma_start", dict(out=flags[:], in_=c_flags), writes=[b_const])
        eps_t = sb("eps_t", [128, 2], F32)
        E("dve", "memset", dict(ap=eps_t[:, 0:1], constant=RMS_EPS), writes=[b_const])
        E("dve", "memset", dict(ap=eps_t[:, 1:2], constant=LN_EPS), writes=[b_const])
        gcol = sb("gcol", [128, KC], F32)
        b_gcol = Buf("gcol")
        ss = sb("ss", [128, 8], F32)
        b_ss = Buf("ss")
        xs = [sb(f"xs{i}", [128, D], F32) for i in range(2)]
        b_xs = [Buf(f"xs{i}") for i in range(2)]
        junk = sb("junk", [128, D], BF16)
        b_junk = Buf("junk")
        xnb = sb("xnb", [128, 2, D], BF16)
        b_xnb = Buf("xnb")
        b_y = [Buf(f"y{i}") for i in range(NT)]
        b_xin = Buf("xin")
        NXIO = 4
        xio = [sb(f"xio{i}", [128, 512], F32) for i in range(NXIO)]
        b_xio = [Buf(f"xio{i}") for i in range(NXIO)]
        xoo = [sb(f"xoo{i}", [128, 512], F32) for i in range(2)]
        b_xoo = [Buf(f"xoo{i}") for i in range(2)]
        xo_slots = Slots(list(zip(xoo, b_xoo)))
        wds = [sb(f"wd{i}", [128, 2, 512], BF16) for i in range(3)]
        b_wds = [Buf(f"wd{i}") for i in range(3)]
        wd_slots = Slots(list(zip(wds, b_wds)))
        sg = [sb(f"sg{i}", [128, TT], F32) for i in range(2)]
        b_sg = [Buf(f"sg{i}") for i in range(2)]

        A1N = (KC * TT) + max(FC * TT, 43 * TT)
        A1 = sb("A1", [128, A1N], BF16)
        A2N = 4 * KC * 256
        A2 = sb("A2", [128, A2N], BF16)

        def view(arena, off_b, shape, dt):
            esz = 4 if dt is F32 else 2
            n = 1
            for s_ in shape[1:]:
                n *= s_
            assert off_b % 4 == 0
            a = arena[:, off_b // 2: off_b // 2 + n * esz // 2]
            if dt is F32:
                a = a.bitcast(F32)
            if len(shape) == 3:
                a = a.rearrange("p (a b) -> p a b", a=shape[1])
            elif len(shape) == 4:
                a = a.rearrange("p (a b c) -> p a b c", a=shape[1], b=shape[2])
            return a

        KB = 1024
        xnT = view(A1, 0, [128, KC, TT], BF16)
        b_xnT = Buf("xnT")
        hT = view(A1, 32 * KB, [128, FC, TT], BF16)
        b_hT = Buf("hT")
        wgs = [view(A2, i * 8 * KB, [128, KC, 256], BF16) for i in range(2)]
        wus = [view(A2, (2 + i) * 8 * KB, [128, KC, 256], BF16) for i in range(2)]
        b_wgs = [Buf(f"wg{i}") for i in range(2)]
        b_wus = [Buf(f"wu{i}") for i in range(2)]
        wg_slots = Slots(list(zip(wgs, b_wgs)))
        wu_slots = Slots(list(zip(wus, b_wus)))
        state = {"a1": [b_xnT, b_hT], "a2": b_wgs + b_wus}

        def sw(region, new):
            switch_alias(state[region], new)
            state[region] = list(new)

        def b_src_of(src, row0):
            if src is x_in:
                return b_xin
            return b_y[row0 // TT]

        def load_gcol(vec_ap):
            E("sync", "dma_start", dict(out=gcol[:], in_=vec_ap.rearrange("o (kc p) -> p (o kc)", p=128),
                                        allow_slow_non_contiguous=True), writes=[b_gcol])

        def norm_rows(src, row0, slot, col):
            x_t, b_x = xs[slot], b_xs[slot]
            E("sync", "dma_start", dict(out=x_t[:], in_=src[row0:row0 + 128, :]),
              reads=[b_src_of(src, row0)], writes=[b_x])
            E("dve", "tensor_tensor", dict(out=junk[:], in0=x_t[:], in1=x_t[:], op=ALU.mult), reads=[b_x], writes=[b_junk])
            E("dve", "reduce_sum", dict(out=ss[:, col:col + 1], in_=junk[:], axis=AX.X), reads=[b_junk], writes=[b_ss])
            E("act", "activation", dict(out=ss[:, col:col + 1], in_=ss[:, col:col + 1], func=AF.Sqrt, bias=eps_t[:, 0:1], scale=1.0 / D),
              reads=[b_ss, b_const], writes=[b_ss])
            E("dve", "reciprocal", dict(out=ss[:, col:col + 1], in_=ss[:, col:col + 1]), reads=[b_ss], writes=[b_ss])
            return x_t, b_x

        def norm_tile(src, t):
            cnt = 0
            for q4 in range(TT // 256):
                for i in range(2):
                    row0 = t * TT + q4 * 256 + i * 128
                    x_t, b_x = norm_rows(src, row0, i, i)
                    E("dve", "tensor_scalar", dict(out=xnb[:, i, :], in0=x_t[:], scalar1=ss[:, i:i + 1], scalar2=None, op0=ALU.mult),
                      reads=[b_x, b_ss], writes=[b_xnb])
                for kc2 in range(KC // 2):
                    bk = bank()
                    for u in range(2):
                        kc = kc2 * 2 + u
                        for i in range(2):
                            E("pe", "matmul", dict(out=ps[:, bk, u * 256 + i * 128:u * 256 + (i + 1) * 128],
                                                   lhsT=xnb[:, i, kc * 128:(kc + 1) * 128], rhs=ident[:], start=True, stop=True),
                              reads=[b_xnb, b_const], writes=[psb[bk]], sig=(u == 1 and i == 1))
                    for u in range(2):
                        kc = kc2 * 2 + u
                        dst = xnT[:, kc, q4 * 256:(q4 + 1) * 256]
                        src_ps = ps[:, bk, u * 256:(u + 1) * 256]
                        if cnt % 2 == 0:
                            E("act", "activation", dict(out=dst, in_=src_ps, func=AF.Copy, scale=gcol[:, kc:kc + 1]),
                              reads=[psb[bk], b_gcol], writes=[b_xnT])
                        else:
                            E("dve", "tensor_scalar", dict(out=dst, in0=src_ps, scalar1=gcol[:, kc:kc + 1], scalar2=None, op0=ALU.mult),
                              reads=[psb[bk], b_gcol], writes=[b_xnT])
                        cnt += 1

        def load_w_fm(slot, wdram, c0, cw):
            t_, b_ = slot
            E("pool", "dma_start", dict(out=t_[:, :, 0:cw], in_=wdram.rearrange("(kc p) n -> p kc n", p=128)[:, :, c0:c0 + cw]),
              writes=[b_])

        def proj_tm(actT, b_act, nkc, wdram, scale, src, t):
            nts = TT // 128
            for cg in range(D // 512):
                accs = list(range(8))
                cs = slice(cg * 512, (cg + 1) * 512)

                def ld(ts):
                    row0 = t * TT + ts * 128
                    E("sync", "dma_start", dict(out=xio[ts % NXIO][:], in_=src[row0:row0 + 128, cs]),
                      reads=[b_src_of(src, row0)], writes=[b_xio[ts % NXIO]])
                for ts in range(NXIO):
                    ld(ts)
                for c0 in range(0, nkc, 2):
                    cn = min(2, nkc - c0)
                    wt, b_w = wd_slots.next()
                    E("pool", "dma_start", dict(
                        out=wt[:, 0:cn, :],
                        in_=wdram[c0 * 128:(c0 + cn) * 128, cs].rearrange("(c p) n -> p c n", p=128)),
                      writes=[b_w])
                    for c in range(c0, c0 + cn):
                        for ts in range(nts):
                            E("pe", "matmul", dict(out=ps[:, accs[ts], :], lhsT=actT[:, c, ts * 128:(ts + 1) * 128], rhs=wt[:, c - c0, :],
                                                   start=(c == 0), stop=(c == nkc - 1)),
                              reads=list(b_act) + [b_w], writes=[psb[accs[ts]]],
                              sig=(c == nkc - 1) or (c == c0 + cn - 1 and ts == nts - 1))
                for ts in range(nts):
                    row0 = t * TT + ts * 128
                    xo, b_xo = xo_slots.next()
                    E("dve", "scalar_tensor_tensor", dict(
                        out=xo[:], in0=ps[:, accs[ts], :], scalar=float(scale), in1=xio[ts % NXIO][:], op0=ALU.mult, op1=ALU.add),
                      reads=[psb[accs[ts]], b_xio[ts % NXIO]], writes=[b_xo])
                    E("sync", "dma_start", dict(out=y_out[row0:row0 + 128, cs], in_=xo[:]), reads=[b_xo], writes=[])
                    dram_written(b_y[t])
                    if ts + NXIO < nts:
                        ld(ts + NXIO)
            pstate["i"] = 0

        def ffn(layer, lnv, wg, wu, wd, src):
            sw("a1", [b_xnT, b_hT])
            sw("a2", b_wgs + b_wus)
            load_gcol(lnv[layer:layer + 1, :])
            for t in range(NT):
                norm_tile(src, t)
                pstate["i"] = 0
                j = 0
                for (c0, cw) in ff_slabs(DFF, 256):
                    sg_ = wg_slots.next()
                    su_ = wu_slots.next()
                    load_w_fm(sg_, wg[layer], c0, cw)
                    load_w_fm(su_, wu[layer], c0, cw)
                    for jj in range(cw // 128):
                        base = (j % 2) * 4
                        for kc in range(KC):
                            for (wslot, off) in ((sg_, 0), (su_, 2)):
                                for half in range(2):
                                    bk = base + off + half
                                    E("pe", "matmul", dict(out=ps[:, bk, :], lhsT=wslot[0][:, kc, jj * 128:(jj + 1) * 128],
                                                           rhs=xnT[:, kc, half * 512:(half + 1) * 512], start=(kc == 0), stop=(kc == KC - 1)),
                                      reads=[wslot[1], b_xnT], writes=[psb[bk]], sig=(kc == KC - 1))
                        s_t, b_s = sg[j % 2], b_sg[j % 2]
                        E("act", "activation", dict(
                            out=s_t[:].rearrange("p (a b) -> p a b", a=2), in_=ps[:, base:base + 2, :], func=AF.Silu),
                          reads=[psb[base], psb[base + 1]], writes=[b_s])
                        E("dve", "tensor_tensor", dict(
                            out=hT[:, j, :].rearrange("p (a b) -> p a b", a=2), in0=s_t[:].rearrange("p (a b) -> p a b", a=2),
                            in1=ps[:, base + 2:base + 4, :], op=ALU.mult),
                          reads=[b_s, psb[base + 2], psb[base + 3]], writes=[b_hT])
                        j += 1
                proj_tm(hT, [b_hT], FC, wd[layer], 0.5, src, t)

        AB = view(A1, 32 * KB, [128, 4, 256], BF16)
        wf_t = view(A1, 34 * KB, [128, 4, 128], F32)
        uaT = view(A1, 36 * KB, [128, TT], BF16)
        pqst = view(A1, 38 * KB, [128, 8, 1024], BF16)
        cvh = view(A1, 54 * KB, [128, 4, TT], F32)
        st4 = [view(A1, (70 + 8 * i) * KB, [128, 4, TT], BF16) for i in range(2)]
        vst = view(A1, 86 * KB, [128, 8, 512], BF16)
        b_AB, b_wf, b_uaT, b_pqst, b_cvh, b_vst = Buf("AB"), Buf("wf"), Buf("uaT"), Buf("pqst"), Buf("cvh"), Buf("vst")
        b_st4 = [Buf("st4a"), Buf("st4b")]
        st4_slots = Slots(list(zip(st4, b_st4)))

        def p2(layer, src):
            sw("a1", [b_xnT, b_AB, b_wf, b_uaT, b_pqst, b_cvh, b_vst] + b_st4)
            sw("a2", b_wgs + b_wus)
            load_gcol(ln_mix[layer:layer + 1, :])
            E("sync", "dma_start", dict(out=wf_t, in_=w_fourier[layer].rearrange("g e f -> e g f")), writes=[b_wf])
            for (tbl, col0) in ((cc_t, 0), (sc_t, 128)):
                bk = bank()
                for g in range(4):
                    E("pe", "matmul", dict(out=ps[:, bk, g * 128:(g + 1) * 128], lhsT=tbl[:], rhs=wf_t[:, g, :], start=True, stop=True),
                      reads=[b_const, b_wf], writes=[psb[bk]], sig=(g == 3))
                E("act", "activation", dict(out=AB[:, :, col0:col0 + 128], in_=ps[:, bk, :].rearrange("p (g f) -> p g f", g=4), func=AF.Copy),
                  reads=[psb[bk]], writes=[b_AB])
            w_l = w_in[layer]
            for t in range(NT):
                norm_tile(src, t)
                tok0 = t * TT
                cur = {}
                for sl in range(IN_COLS // 256):
                    c0 = sl * 256
                    part = c0 // WM
                    if part == 5:
                        continue
                    wsl = wg_slots.next() if sl % 2 == 0 else wu_slots.next()
                    load_w_fm(wsl, w_l, c0, 256)
                    for jj in range(2):
                        cidx = sl * 2 + jj
                        j4 = cidx % 4
                        bk = banks2()
                        for kc in range(KC):
                            for half in range(2):
                                E("pe", "matmul", dict(out=ps[:, bk + half, :], lhsT=wsl[0][:, kc, jj * 128:(jj + 1) * 128],
                                                       rhs=xnT[:, kc, half * 512:(half + 1) * 512], start=(kc == 0), stop=(kc == KC - 1)),
                                  reads=[wsl[1], b_xnT], writes=[psb[bk + half]], sig=(kc == KC - 1))
                        pin = ps[:, bk:bk + 2, :]
                        pbufs = [psb[bk], psb[bk + 1]]

                        def r2(ap):
                            return ap.rearrange("p (a b) -> p a b", a=2)
                        if part == 0:
                            E("act", "activation", dict(out=r2(uaT), in_=pin, func=AF.Copy), reads=pbufs, writes=[b_uaT])
                            for ts2 in range(4):
                                bq = bank()
                                for u in range(2):
                                    ts = ts2 * 2 + u
                                    E("pe", "matmul", dict(out=ps[:, bq, u * 256:(u + 1) * 256], lhsT=uaT[:, ts * 128:(ts + 1) * 128],
                                                           rhs=AB[:, j4, :], start=True, stop=True),
                                      reads=[b_uaT, b_AB], writes=[psb[bq]], sig=(u == 1))
                                E("dve", "tensor_copy", dict(out=pqst[:, ts2 * 2:ts2 * 2 + 2, j4 * 256:(j4 + 1) * 256],
                                                             in_=ps[:, bq, :].rearrange("p (u c) -> p u c", u=2)),
                                  reads=[psb[bq]], writes=[b_pqst])
                            if j4 == 3:
                                E("sync", "dma_start", dict(out=s_pq[tok0:tok0 + TT, :].rearrange("(ts p) c -> p ts c", p=128), in_=pqst),
                                  reads=[b_pqst], writes=[])
                                dram_written(b_spq[t])
                        elif part in (1, 7):
                            E("act", "activation", dict(out=r2(cvh[:, j4, :]), in_=pin, func=AF.Copy), reads=pbufs, writes=[b_cvh])
                        else:
                            if j4 == 0:
                                cur["st"] = st4_slots.next()
                            stt, b_stt = cur["st"]
                            dst = r2(stt[:, j4, :])
                            if part == 2:
                                s_t, b_s = sg[j4 % 2], b_sg[j4 % 2]
                                E("act", "activation", dict(out=r2(s_t[:]), in_=pin, func=AF.Sigmoid), reads=pbufs, writes=[b_s])
                                E("dve", "tensor_tensor", dict(out=dst, in0=r2(cvh[:, j4, :]), in1=r2(s_t[:]), op=ALU.mult),
                                  reads=[b_s, b_cvh], writes=[b_stt])
                            elif part == 3:
                                E("act", "activation", dict(out=dst, in_=pin, func=AF.Copy, scale=0.125), reads=pbufs, writes=[b_stt])
                            elif part == 8:
                                E("dve", "tensor_tensor", dict(out=dst, in0=r2(cvh[:, j4, :]), in1=pin, op=ALU.mult),
                                  reads=pbufs + [b_cvh], writes=[b_stt])
                            else:
                                E("dve", "tensor_copy", dict(out=dst, in_=pin), reads=pbufs, writes=[b_stt])
                            if j4 == 3:
                                dmap = {2: (s_hb, PADC, b_shb), 3: (s_q, 0, b_sq), 4: (s_k, 0, b_sk), 6: (s_sb, 0, b_ssb), 8: (s_p, PADC, b_sp)}
                                dt_, off, bl = dmap[part]
                                E("sync", "dma_start", dict(out=dt_[:, off + tok0:off + tok0 + TT].rearrange("(j p) n -> p j n", p=128), in_=stt),
                                  reads=[b_stt], writes=[])
                                dram_written(bl[t])
                wv0 = wg_slots.next()
                wv1 = wu_slots.next()
                load_w_fm(wv0, w_l, 5 * WM, 256)
                load_w_fm(wv1, w_l, 5 * WM + 256, 256)
                for ts in range(TT // 128):
                    bk = bank()
                    for hc, wv in enumerate((wv0, wv1)):
                        for kc in range(KC):
                            E("pe", "matmul", dict(out=ps[:, bk, hc * 256:(hc + 1) * 256], lhsT=xnT[:, kc, ts * 128:(ts + 1) * 128],
                                                   rhs=wv[0][:, kc, :], start=(kc == 0), stop=(kc == KC - 1)),
                              reads=[wv[1], b_xnT], writes=[psb[bk]], sig=(kc == KC - 1))
                    if ts % 2 == 0:
                        E("act", "activation", dict(out=vst[:, ts, :], in_=ps[:, bk, :], func=AF.Copy), reads=[psb[bk]], writes=[b_vst])
                    else:
                        E("dve", "tensor_copy", dict(out=vst[:, ts, :], in_=ps[:, bk, :]), reads=[psb[bk]], writes=[b_vst])
                E("sync", "dma_start", dict(out=s_v[tok0:tok0 + TT, :].rearrange("(ts p) c -> p ts c", p=128), in_=vst),
                  reads=[b_vst], writes=[])
                dram_written(b_sv[t])

        yT = view(A1, 0, [128, KC, L], BF16)
        b_yT = [Buf(f"yT{i}") for i in range(4)]
        A1R = 64 * KB
        TAIL = A1R + 52 * KB
        cw_t = view(A1, TAIL, [128, 4, CONF_K], F32)
        cb_t = view(A1, TAIL + 512, [128, 4], F32)
        lg_t = view(A1, TAIL + 528, [128, 4], F32)
        lb_t = view(A1, TAIL + 544, [128, 4], F32)
        dw_t = view(A1, TAIL + 560, [128, 4, 3], F32)
        b_small = Buf("small")
        ones_fl = view(A1, TAIL + 1024, [128, 2, 128], BF16)
        b_ofl = Buf("ones_fl")
        f_cs = [view(A2, i * 4 * KB, [128, 4, 512], BF16) for i in range(2)]
        f_sn = [view(A2, (8 + i * 4) * KB, [128, 4, 512], BF16) for i in range(2)]
        f_pq = [view(A2, (16 + i * 8) * KB, [128, 4, 1024], BF16) for i in range(2)]
        b_fcs = [Buf(f"fcs{i}") for i in range(2)]
        b_fsn = [Buf(f"fsn{i}") for i in range(2)]
        b_fpq = [Buf(f"fpq{i}") for i in range(2)]
        HBW = TT + 32
        c_hb = view(A1, A1R, [128, 4, HBW], BF16)
        c_acc = view(A1, A1R + 9 * KB, [128, 4, TT], F32)
        c_sq = view(A1, A1R + 25 * KB, [128, 4, 512], F32)
        c_mean = view(A1, A1R + 33 * KB, [128, 512], F32)
        c_rstd = view(A1, A1R + 35 * KB, [128, 512], F32)
        c_msq = view(A1, A1R + 37 * KB, [128, 512], F32)
        c_t1 = [view(A1, A1R + (39 + 2 * i) * KB, [128, 512], F32) for i in range(2)]
        b_chb, b_csq, b_cmean, b_crstd, b_cmsq = Buf("chb"), Buf("csq"), Buf("cmean"), Buf("crstd"), Buf("cmsq")
        b_cacc = [Buf(f"cacc{i}") for i in range(4)]
        b_ct1 = [Buf("ct1a"), Buf("ct1b")]
        conf_bufs = [b_chb, b_csq, b_cmean, b_crstd, b_cmsq] + b_cacc + b_ct1
        d_p = view(A2, 0, [128, 4, HBW], BF16)
        d_sb = view(A2, 9 * KB, [128, 4, TT], BF16)
        d_acc = [view(A2, (17 + 4 * i) * KB, [128, TT], F32) for i in range(2)]
        b_dp, b_dsb = Buf("dp"), Buf("dsb")
        b_dacc = [Buf("dacc0"), Buf("dacc1")]
        a_kT = view(A2, 0, [128, 3 * L], BF16)
        a_qT = view(A2, 12 * KB, [128, L], BF16)
        a_pT = [view(A2, (16 + 3 * i) * KB, [128, 3, 4, 128], BF16) for i in range(2)]
        a_et = [view(A2, (22 + 2 * i) * KB, [128, 512], F32) for i in range(3)]
        b_akT, b_aqT = Buf("akT"), Buf("aqT")
        b_apT = [Buf("apT0"), Buf("apT1")]
        b_aet = [Buf(f"aet{i}") for i in range(3)]
        a_vt = view(A1, A1R, [128, 48, 128], BF16)
        a_num = view(A1, A1R + 12 * KB, [128, L], F32)
        a_den = view(A1, A1R + 20 * KB, [128, L], F32)
        a_mk = view(A1, A1R + 28 * KB, [128, 2, 3, 128], F32)
        b_avt, b_anum, b_aden, b_amk = Buf("avt"), Buf("anum"), Buf("aden"), Buf("amk")
        attn_a1 = [b_avt, b_anum, b_aden, b_amk]
        attn_a2 = [b_akT, b_aqT] + b_apT + b_aet

        def seg_tiles(s):
            return [2 * s, 2 * s + 1]

        def load_small(layer):
            nck = dict(allow_slow_non_contiguous=True)
            for c4 in range(4):
                E("sync", "dma_start", dict(out=cw_t[:, c4, :], in_=conv_b_w[layer][:, c4 * 128:(c4 + 1) * 128].rearrange("j p -> p j"), **nck),
                  writes=[b_small])
                E("sync", "dma_start", dict(out=dw_t[:, c4, :], in_=conv_d_w[layer][:, c4 * 128:(c4 + 1) * 128].rearrange("j p -> p j"), **nck),
                  writes=[b_small])
            for (t_, src_) in ((cb_t, conv_b_bias), (lg_t, ln_conv_gain), (lb_t, ln_conv_bias)):
                E("sync", "dma_start", dict(out=t_, in_=src_[layer:layer + 1, :].rearrange("o (c p) -> p (o c)", p=128), **nck), writes=[b_small])

        def fourier(s):
            for stile in range(L // 512):
                s0 = s * L + stile * 512
                base = (stile % 2) * 4
                ntc = TOK // 128
                for tc0 in range(0, ntc, 4):
                    i2 = (tc0 // 4) % 2
                    rows = slice(tc0 * 128, (tc0 + 4) * 128)
                    E("sync", "dma_start", dict(out=f_cs[i2], in_=c_dftc[rows, s0:s0 + 512].rearrange("(c p) n -> p c n", p=128)), writes=[b_fcs[i2]])
                    E("sync", "dma_start", dict(out=f_sn[i2], in_=c_dfts[rows, s0:s0 + 512].rearrange("(c p) n -> p c n", p=128)), writes=[b_fsn[i2]])
                    E("sync", "dma_start", dict(out=f_pq[i2], in_=s_pq[rows, :].rearrange("(c p) n -> p c n", p=128)),
                      reads=[b_spq[(tc0 * 128) // TT]], writes=[b_fpq[i2]])
                    for c in range(4):
                        first = (tc0 + c == 0)
                        last = (tc0 + c == ntc - 1)
                        for g in range(4):
                            E("pe", "matmul", dict(out=ps[:, base + g, :], lhsT=f_pq[i2][:, c, g * 256:g * 256 + 128], rhs=f_cs[i2][:, c, :],
                                                   start=first, stop=False),
                              reads=[b_fpq[i2], b_fcs[i2]], writes=[psb[base + g]], sig=False)
                            E("pe", "matmul", dict(out=ps[:, base + g, :], lhsT=f_pq[i2][:, c, g * 256 + 128:g * 256 + 256], rhs=f_sn[i2][:, c, :],
                                                   start=False, stop=last),
                              reads=[b_fpq[i2], b_fsn[i2]], writes=[psb[base + g]], sig=(last or (c == 3 and g == 3)))
                for g in range(4):
                    dst = yT[:, g, stile * 512:(stile + 1) * 512]
                    if g % 2 == 0:
                        E("act", "activation", dict(out=dst, in_=ps[:, base + g, :], func=AF.Copy), reads=[psb[base + g]], writes=[b_yT[0]])
                    else:
                        E("dve", "tensor_copy", dict(out=dst, in_=ps[:, base + g, :]), reads=[psb[base + g]], writes=[b_yT[0]])

        def halo_flags(buf_ap, b_, s, half, width):
            if half == 0:
                E("dve", "tensor_scalar", dict(out=buf_ap[:, :, 0:PADC], in0=buf_ap[:, :, 0:PADC], scalar1=flags[:, 2 * s:2 * s + 1],
                                               scalar2=None, op0=ALU.mult), reads=[b_, b_const], writes=[b_])
            else:
                E("dve", "tensor_scalar", dict(out=buf_ap[:, :, PADC + TT:PADC + TT + PADC], in0=buf_ap[:, :, PADC + TT:PADC + TT + PADC],
                                               scalar1=flags[:, 2 * s + 1:2 * s + 2], scalar2=None, op0=ALU.mult), reads=[b_, b_const], writes=[b_])

        def conf_conv(s, half):
            h0 = s * L + half * TT
            tl = [x_ for x_ in (h0 // TT - 1, h0 // TT, h0 // TT + 1) if 0 <= x_ < NT]
            E("sync", "dma_start", dict(out=c_hb, in_=s_hb[:, h0:h0 + HBW].rearrange("(c p) n -> p c n", p=128)),
              reads=[b_shb[x_] for x_ in tl] + [b_pad], writes=[b_chb])
            halo_flags(c_hb, b_chb, s, half, HBW)
            o = PADC - CONF_K // 2
            for j in range(CONF_K):
                for c4 in range(4):
                    if j == 0:
                        E("dve", "tensor_scalar", dict(out=c_acc[:, c4, :], in0=c_hb[:, c4, o:o + TT], scalar1=cw_t[:, c4, 0:1],
                                                       scalar2=cb_t[:, c4:c4 + 1], op0=ALU.mult, op1=ALU.add),
                          reads=[b_chb, b_small], writes=[b_cacc[c4]])
                    else:
                        E("dve", "scalar_tensor_tensor", dict(out=c_acc[:, c4, :], in0=c_hb[:, c4, o + j:o + j + TT], scalar=cw_t[:, c4, j:j + 1],
                                                              in1=c_acc[:, c4, :], op0=ALU.mult, op1=ALU.add),
                          reads=[b_chb, b_small, b_cacc[c4]], writes=[b_cacc[c4]])

        def conf_ln(s, half):
            for tl in range(TT // 512):
                cs = slice(tl * 512, (tl + 1) * 512)
                E("act", "activation", dict(out=c_sq, in_=c_acc[:, :, cs], func=AF.Square), reads=b_cacc, writes=[b_csq])
                b1 = bank()
                b2 = bank()
                for c4 in range(4):
                    E("pe", "matmul", dict(out=ps[:, b1, :], lhsT=ones_f[:], rhs=c_acc[:, c4, cs], start=(c4 == 0), stop=(c4 == 3)),
                      reads=[b_const, b_cacc[c4]], writes=[psb[b1]], sig=(c4 == 3))
                for c4 in range(4):
                    E("pe", "matmul", dict(out=ps[:, b2, :], lhsT=ones_f[:], rhs=c_sq[:, c4, :], start=(c4 == 0), stop=(c4 == 3)),
                      reads=[b_const, b_csq], writes=[psb[b2]], sig=(c4 == 3))
                E("act", "activation", dict(out=c_mean, in_=ps[:, b1, :], func=AF.Copy, scale=1.0 / WM), reads=[psb[b1]], writes=[b_cmean])
                E("dve", "tensor_tensor", dict(out=c_msq, in0=c_mean, in1=c_mean, op=ALU.mult), reads=[b_cmean], writes=[b_cmsq])
                E("dve", "scalar_tensor_tensor", dict(out=c_rstd, in0=ps[:, b2, :], scalar=1.0 / WM, in1=c_msq, op0=ALU.mult, op1=ALU.subtract),
                  reads=[psb[b2], b_cmsq], writes=[b_crstd])
                E("act", "activation", dict(out=c_rstd, in_=c_rstd, func=AF.Sqrt, bias=eps_t[:, 1:2], scale=1.0), reads=[b_crstd, b_const], writes=[b_crstd])
                E("dve", "reciprocal", dict(out=c_rstd, in_=c_rstd), reads=[b_crstd], writes=[b_crstd])
                for c4 in range(4):
                    t1, b_t1 = c_t1[c4 % 2], b_ct1[c4 % 2]
                    E("dve", "tensor_tensor", dict(out=t1, in0=c_acc[:, c4, cs], in1=c_mean, op=ALU.subtract), reads=[b_cacc[c4], b_cmean], writes=[b_t1])
                    E("dve", "tensor_tensor", dict(out=t1, in0=t1, in1=c_rstd, op=ALU.mult), reads=[b_t1, b_crstd], writes=[b_t1])
                    p0 = half * TT + tl * 512
                    E("act", "activation", dict(out=yT[:, 4 + c4, p0:p0 + 512], in_=t1, func=AF.Silu, bias=lb_t[:, c4:c4 + 1], scale=lg_t[:, c4:c4 + 1]),
                      reads=[b_t1, b_small], writes=[b_yT[1]])

        def shortconv(s, half):
            h0 = s * L + half * TT
            tl = [x_ for x_ in (h0 // TT - 1, h0 // TT, h0 // TT + 1) if 0 <= x_ < NT]
            E("sync", "dma_start", dict(out=d_p, in_=s_p[:, h0:h0 + HBW].rearrange("(c p) n -> p c n", p=128)),
              reads=[b_sp[x_] for x_ in tl] + [b_pad], writes=[b_dp])
            E("sync", "dma_start", dict(out=d_sb, in_=s_sb[:, h0:h0 + TT].rearrange("(c p) n -> p c n", p=128)),
              reads=[b_ssb[h0 // TT]], writes=[b_dsb])
            halo_flags(d_p, b_dp, s, half, HBW)
            o = PADC - 1
            for c4 in range(4):
                acc, b_a = d_acc[c4 % 2], b_dacc[c4 % 2]
                E("dve", "tensor_scalar", dict(out=acc, in0=d_p[:, c4, o:o + TT], scalar1=dw_t[:, c4, 0:1], scalar2=None, op0=ALU.mult),
                  reads=[b_dp, b_small], writes=[b_a])
                for j in (1, 2):
                    E("dve", "scalar_tensor_tensor", dict(out=acc, in0=d_p[:, c4, o + j:o + j + TT], scalar=dw_t[:, c4, j:j + 1], in1=acc,
                                                          op0=ALU.mult, op1=ALU.add), reads=[b_dp, b_small, b_a], writes=[b_a])
                E("dve", "tensor_tensor", dict(out=yT[:, 12 + c4, half * TT:(half + 1) * TT], in0=acc, in1=d_sb[:, c4, :], op=ALU.mult),
                  reads=[b_a, b_dsb], writes=[b_yT[3]])

        def attention(s):
            seg0 = s * L
            for j_ in range(2):
                E("dve", "tensor_scalar", dict(out=ones_fl[:, j_, :], in0=ones_bf[:], scalar1=flags[:, 2 * s + j_:2 * s + j_ + 1], scalar2=None,
                                               op0=ALU.mult), reads=[b_const], writes=[b_ofl])
            for c4 in range(4):
                rows = slice(c4 * 128, (c4 + 1) * 128)
                E("sync", "dma_start", dict(out=a_qT, in_=s_q[rows, seg0:seg0 + L]), reads=[b_sq[x_] for x_ in seg_tiles(s)], writes=[b_aqT])
                lo = max(0, seg0 - L)
                hi = min(TOK, seg0 + 2 * L)
                E("sync", "dma_start", dict(out=a_kT[:, lo - seg0 + L:hi - seg0 + L], in_=s_k[rows, lo:hi]),
                  reads=[b_sk[x_] for x_ in range(lo // TT, hi // TT)], writes=[b_akT])
                first_branch = True
                for b, d in enumerate(DILS):
                    nb = L // (128 * d)
                    ngl = NSEG * nb
                    for par in range(2):
                        h = 2 * c4 + par
                        si = 2 * b - h + 7
                        E("sync", "dma_start", dict(out=a_mk[:, par, :, :], in_=c_mask[:, si * 3:si * 3 + 3, :]), writes=[b_amk])
                    gl0 = s * nb - 1
                    glo = max(0, gl0)
                    ghi = min(ngl, s * nb + nb + 1)
                    for r in range(d):
                        ng = ghi - glo
                        src_v = s_v[:, rows].rearrange("(g p r) c -> r p g c", r=d, p=128)[r, :, glo:ghi, :]
                        i0 = r * (nb + 2) + (glo - gl0)
                        t0_ = (glo * 128 * d) // TT
                        t1_ = ((ghi * 128 * d) - 1) // TT
                        E("sync", "dma_start", dict(out=a_vt[:, i0:i0 + ng, :], in_=src_v),
                          reads=[b_sv[x_] for x_ in range(t0_, t1_ + 1)], writes=[b_avt])
                        if glo == gl0:
                            E("dve", "tensor_scalar", dict(out=a_vt[:, i0, :], in0=a_vt[:, i0, :], scalar1=flags[:, 2 * s:2 * s + 1], scalar2=None,
                                                           op0=ALU.mult), reads=[b_avt, b_const], writes=[b_avt])
                        if ghi == s * nb + nb + 1:
                            i1 = r * (nb + 2) + nb + 1
                            E("dve", "tensor_scalar", dict(out=a_vt[:, i1, :], in0=a_vt[:, i1, :], scalar1=flags[:, 2 * s + 1:2 * s + 2], scalar2=None,
                                                           op0=ALU.mult), reads=[b_avt, b_const], writes=[b_avt])
                    blocks = [(r, a) for r in range(d) for a in range(nb)]
                    for par in range(2):
                        pr = slice(par * 64, par * 64 + 64)
                        for g0 in range(0, len(blocks), 4):
                            grp = blocks[g0:g0 + 4]
                            pT, b_pT = a_pT[(g0 // 4) % 2], b_apT[(g0 // 4) % 2]
                            sbk = [bank() for _ in range(3)]
                            exist = {}
                            for ri, rel in enumerate((-1, 0, 1)):
                                any_ = False
                                for i, (r, a) in enumerate(grp):
                                    gk = s * nb + a + rel
                                    ok = 0 <= gk < ngl
                                    exist[(ri, i)] = ok
                                    if not ok:
                                        continue
                                    any_ = True
                                    kst = r + d * 128 * gk - seg0 + L
                                    qst = r + d * 128 * a
                                    E("pe", "matmul", dict(out=ps[:, sbk[ri], i * 128:(i + 1) * 128],
                                                           lhsT=a_kT[pr, kst:kst + 127 * d + 1:d], rhs=a_qT[pr, qst:qst + 127 * d + 1:d], start=True, stop=True),
                                      reads=[b_akT, b_aqT], writes=[psb[sbk[ri]]], sig=True)
                                if not any_:
                                    continue
                                et, b_et = a_et[ri], b_aet[ri]
                                ex = [i for i in range(len(grp)) if exist[(ri, i)]]
                                i_lo, i_hi = ex[0], ex[-1] + 1
                                assert ex == list(range(i_lo, i_hi))
                                ni = i_hi - i_lo
                                E("act", "activation", dict(out=et[:, i_lo * 128:i_hi * 128], in_=ps[:, sbk[ri], i_lo * 128:i_hi * 128], func=AF.Exp),
                                  reads=[psb[sbk[ri]]], writes=[b_et])
                                E("dve", "tensor_tensor", dict(out=pT[:, ri, i_lo:i_hi, :], in0=et[:, i_lo * 128:i_hi * 128].rearrange("p (i q) -> p i q", i=ni),
                                                               in1=a_mk[:, par, ri:ri + 1, :].broadcast_to([128, ni, 128]), op=ALU.mult),
                                  reads=[b_et, b_amk], writes=[b_pT])
                            nbk = bank()
                            dbk = bank()
                            for i, (r, a) in enumerate(grp):
                                rels = [ri for ri in range(3) if exist[(ri, i)]]
                                for (bk_, is_den) in ((nbk, False), (dbk, True)):
                                    for n_, ri in enumerate(rels):
                                        gk = s * nb + a + (ri - 1)
                                        vi = r * (nb + 2) + (gk - gl0)
                                        if is_den:
                                            lhs = ones_fl[:, 0, :] if gk < s * nb else (ones_fl[:, 1, :] if gk >= (s + 1) * nb else ones_bf[:])
                                        else:
                                            lhs = a_vt[:, vi, :]
                                        E("pe", "matmul", dict(out=ps[:, bk_, i * 128:(i + 1) * 128], lhsT=lhs, rhs=pT[:, ri, i, :],
                                                               start=(n_ == 0), stop=(n_ == len(rels) - 1)),
                                          reads=[b_pT, b_avt, b_const, b_ofl], writes=[psb[bk_]], sig=(n_ == len(rels) - 1))
                            r0, a0 = grp[0]
                            ng_ = len(grp)
                            for (bk_, accv, b_acc) in ((nbk, a_num, b_anum), (dbk, a_den, b_aden)):
                                if d == 1:
                                    dst = accv[pr, a0 * 128:a0 * 128 + ng_ * 128]
                                    srcp = ps[pr, bk_, 0:ng_ * 128]
                                elif d == 4:
                                    dst = accv[pr, r0:r0 + 4 * (128 * ng_ - 1) + 1:4]
                                    srcp = ps[pr, bk_, 0:ng_ * 128]
                                else:
                                    dst = accv[pr, :].rearrange("q (p r) -> q r p", r=16)[:, r0:r0 + ng_, :]
                                    srcp = ps[pr, bk_, 0:ng_ * 128].rearrange("q (i p) -> q i p", i=ng_)
                                if first_branch:
                                    E("act", "activation", dict(out=dst, in_=srcp, func=AF.Copy), reads=[psb[bk_]], writes=[b_acc])
                                else:
                                    E("dve", "tensor_tensor", dict(out=dst, in0=dst, in1=srcp, op=ALU.add), reads=[psb[bk_], b_acc], writes=[b_acc])
                    first_branch = False
                E("dve", "reciprocal", dict(out=a_den, in_=a_den), reads=[b_aden], writes=[b_aden])
                E("dve", "tensor_tensor", dict(out=yT[:, 8 + c4, :], in0=a_num, in1=a_den, op=ALU.mult), reads=[b_anum, b_aden], writes=[b_yT[2]])

        def zero_pads():
            E("dve", "memset", dict(ap=junk[:, 0:PADC], constant=0.0), writes=[b_junk])
            for dt_ in (s_hb, s_p):
                for off in (0, PADC + TOK):
                    for c4 in range(4):
                        E("sync", "dma_start", dict(out=dt_[c4 * 128:(c4 + 1) * 128, off:off + PADC], in_=junk[:, 0:PADC]), reads=[b_junk], writes=[])
                        dram_written(b_pad)

        def p3(layer, src):
            all_y = list(b_yT)
            for s in range(NSEG):
                sw("a1", all_y + conf_bufs + [b_small, b_ofl])
                sw("a2", b_fcs + b_fsn + b_fpq)
                for mi, mx in enumerate("abcd"):
                    if mx not in mixers:
                        E("dve", "memset", dict(ap=yT[:, 4 * mi:4 * mi + 4, :], constant=0.0), writes=[b_yT[mi]])
                if s == 0:
                    load_small(layer)
                else:
                    pass
                for half in range(2):
                    if "b" in mixers:
                        conf_conv(s, half)
                    if half == 0 and "a" in mixers:
                        fourier(s)
                    if "b" in mixers:
                        conf_ln(s, half)
                pstate["i"] = 0
                if "d" in mixers:
                    sw("a2", [b_dp, b_dsb] + b_dacc)
                    for half in range(2):
                        shortconv(s, half)
                if "c" in mixers:
                    sw("a2", attn_a2)
                    sw("a1", all_y + attn_a1 + [b_small, b_ofl])
                    attention(s)
                pstate["i"] = 0
                for half in range(2):
                    proj_tm(yT[:, :, half * TT:(half + 1) * TT], all_y, KC, w_out[layer], 1.0, src, 2 * s + half)

        def final_norm(src):
            gt = view(A1, 0, [128, D], F32)
            fo = [view(A1, (8 + 8 * i) * KB, [128, D], F32) for i in range(2)]
            b_gt = Buf("gt")
            b_fo = [Buf("fo0"), Buf("fo1")]
            sw("a1", [b_gt] + b_fo)
            E("sync", "dma_start", dict(out=gt, in_=ln_final[0:1, :].partition_broadcast(128)), writes=[b_gt])
            for r in range(TOK // 128):
                row0 = r * 128
                c = r % 8
                x_t, b_x = norm_rows(src, row0, r % 2, c)
                E("dve", "scalar_tensor_tensor", dict(out=fo[r % 2], in0=x_t[:], scalar=ss[:, c:c + 1], in1=gt, op0=ALU.mult, op1=ALU.mult),
                  reads=[b_x, b_ss, b_gt], writes=[b_fo[r % 2]])
                E("sync", "dma_start", dict(out=y_out[row0:row0 + 128, :], in_=fo[r % 2]), reads=[b_fo[r % 2]], writes=[])
                dram_written(b_y[row0 // TT])

        zero_pads()
        src = x_in
        for layer in range(DEPTH):
            if "f1" in stages:
                ffn(layer, ln_ffn1, w1g, w1u, w1d, src)
                src = y_out
            if "mix" in stages:
                p2(layer, src)
                p3(layer, src)
                src = y_out
            if "f2" in stages:
                ffn(layer, ln_ffn2, w2g, w2u, w2d, src)
                src = y_out
        final_norm(src)
        k.finish(b_y)
        print("instructions emitted:", k.nins)
        with nc.Block() as block:
            k.replay(block)
    return nc


_CONST_CACHE = {}


def make_consts(NSEG, joined):
    key = (NSEG, bool(joined))
    if key in _CONST_CACHE:
        return _CONST_CACHE[key]
    TOK = NSEG * L
    S = TOK if joined else L
    idx = np.arange(S, dtype=np.int64)
    m = (idx[:, None] * idx[None, :]) % S
    ang = (2.0 * np.pi / S) * m.astype(np.float64)
    c = (np.cos(ang) / np.sqrt(S)).astype(np.float32)
    sn = (-np.sin(ang) / np.sqrt(S)).astype(np.float32)
    del ang, m
    if joined:
        C, Sn = c, sn
    else:
        C = np.zeros((TOK, TOK), np.float32)
        Sn = np.zeros((TOK, TOK), np.float32)
        for i in range(NSEG):
            C[i * L:(i + 1) * L, i * L:(i + 1) * L] = c
            Sn[i * L:(i + 1) * L, i * L:(i + 1) * L] = sn
    ce = np.arange(128, dtype=np.int64)
    a2 = (2.0 * np.pi / 128) * ((ce[:, None] * ce[None, :]) % 128).astype(np.float64)
    cc = (np.cos(a2) / np.sqrt(128.0)).astype(np.float32)
    sc = (np.sin(a2) / np.sqrt(128.0)).astype(np.float32)
    kk = np.arange(128)[:, None]
    qq = np.arange(128)[None, :]
    mask = np.zeros((128, NSLOPE * 3, 128), np.float32)
    for si in range(NSLOPE):
        slope = 2.0 ** (si - 8)
        for ri, rel in enumerate((-1, 0, 1)):
            offs = np.abs(128 * rel + kk - qq).astype(np.float64)
            mask[:, si * 3 + ri, :] = np.where(offs <= 64, np.exp(-slope * offs), 0.0).astype(np.float32)
    flags = np.zeros((128, 2 * NSEG), np.float32)
    if joined:
        for s_ in range(NSEG):
            flags[:, 2 * s_] = 1.0 if s_ > 0 else 0.0
            flags[:, 2 * s_ + 1] = 1.0 if s_ < NSEG - 1 else 0.0
    out = dict(
        c_ident=np.eye(128, dtype=np.float32).astype(ml_dtypes.bfloat16),
        c_dftc=C.astype(ml_dtypes.bfloat16), c_dfts=Sn.astype(ml_dtypes.bfloat16),
        c_cc=cc, c_sc=sc, c_mask=mask, c_flags=flags)
    _CONST_CACHE[key] = out
    return out


NSEG_FULL = 4
N_PROMPT_CORES = 4
N_SAMPLE_CORES = 2
_WEIGHT_KEYS = ("ln_ffn1", "w_ffn1_gate", "w_ffn1_up", "w_ffn1_down", "ln_mix", "w_in", "w_fourier", "conv_b_w",
                "conv_b_bias", "ln_conv_gain", "ln_conv_bias", "conv_d_w", "w_out", "ln_ffn2", "w_ffn2_gate",
                "w_ffn2_up", "w_ffn2_down")


def kernel(**inputs):
    xp = np.ascontiguousarray(np.asarray(inputs["x_prompt"], dtype=np.float32))
    xsm = np.ascontiguousarray(np.asarray(inputs["x_sample"], dtype=np.float32))
    depth = int(np.asarray(inputs["ln_ffn1"]).shape[0])
    dff = int(np.asarray(inputs["w_ffn1_gate"]).shape[2])
    assert xp.shape == (16, L, D) and xsm.shape == (2, NSEG_FULL * L, D)
    cfg = dict(NSEG=NSEG_FULL, DEPTH=depth, DFF=dff)
    nc = build(cfg)
    w = {k_: np.ascontiguousarray(np.asarray(inputs[k_], dtype=np.float32)) for k_ in _WEIGHT_KEYS}
    w["ln_final"] = np.ascontiguousarray(np.asarray(inputs["ln_final"], dtype=np.float32)).reshape(1, D)
    cp = make_consts(NSEG_FULL, False)
    cs = make_consts(NSEG_FULL, True)
    in_maps = []
    for c in range(N_PROMPT_CORES):
        in_maps.append(dict(x=xp[4 * c:4 * c + 4].reshape(NSEG_FULL * L, D), **w, **cp))
    for c in range(N_SAMPLE_CORES):
        in_maps.append(dict(x=xsm[c], **w, **cs))
    n = len(in_maps)
    res = run_bass_kernel_spmd(nc, in_maps, core_ids=list(range(n)))
    y_prompt = np.empty((16, L, D), np.float32)
    y_sample = np.empty((2, NSEG_FULL * L, D), np.float32)
    for c in range(N_PROMPT_CORES):
        y_prompt[4 * c:4 * c + 4] = np.asarray(res.results[c]["y"], dtype=np.float32).reshape(4, L, D)
    for c in range(N_SAMPLE_CORES):
        y_sample[c] = np.asarray(res.results[N_PROMPT_CORES + c]["y"], dtype=np.float32)
    return (y_prompt, y_sample)
```
